# Optimizing a Trainium2 kernel written in Bass

```python
import math
import jax, jax.numpy as jnp
from jax import lax
import numpy as np

D_MODEL = 2048
BATCH = 4
SEQ = 4096
DEPTH = 4

N_A = DEPTH // 2
N_B = DEPTH - N_A
MEM_LEN = 256
HEAD_DIM = 128
MEM_HEADS = 4
MEM_W = MEM_HEADS * HEAD_DIM
MIX_W = D_MODEL - MEM_W
CHUNK = 128
GMLP_GROUPS = 6
GMLP_GW = MIX_W // GMLP_GROUPS
DIFF_HEADS = MIX_W // (2 * HEAD_DIM)
DIFF_VDIM = 2 * HEAD_DIM
D_FF = 5632
ROPE_THETA = 10000.0
EPS = 1e-6
Q_BLOCK = 128

kernel_name = "yoco_gmlp_diffattn_macaron_memxattn"


def rmsnorm(x, g):
    x32 = x.astype(jnp.float32)
    y = x32 * lax.rsqrt(jnp.mean(x32 * x32, axis=-1, keepdims=True) + EPS)
    return (y * g.astype(jnp.float32)).astype(x.dtype)


def rope_tables(seq):
    pos = jnp.arange(seq, dtype=jnp.float32)
    inv = ROPE_THETA ** (-jnp.arange(0, HEAD_DIM, 2, dtype=jnp.float32) / HEAD_DIM)
    ang = pos[:, None] * inv[None, :]
    ang = jnp.concatenate([ang, ang], axis=-1)
    return jnp.cos(ang), jnp.sin(ang)


def apply_rope(x, cos, sin):
    x1, x2 = jnp.split(x, 2, axis=-1)
    rot = jnp.concatenate([-x2, x1], axis=-1)
    return (x * cos + rot * sin).astype(x.dtype)


def swiglu(h, w_gu, w_down):
    g, u = jnp.split(h @ w_gu, 2, axis=-1)
    return (jax.nn.silu(g) * u) @ w_down


def gmlp_mix(z, v_norm_g, w_s, b_s):
    B, S, _ = z.shape
    u, v = jnp.split(z, 2, axis=-1)
    v = rmsnorm(v, v_norm_g)
    v = v.reshape(B, S // CHUNK, CHUNK, GMLP_GROUPS, GMLP_GW)
    causal = jnp.tril(jnp.ones((CHUNK, CHUNK), dtype=bool))
    w = jnp.where(causal[None], w_s, jnp.zeros_like(w_s))
    mixed = jnp.einsum('gts,bcsgd->bctgd', w, v) + jnp.transpose(b_s)[None, None, :, :, None]
    return u * mixed.reshape(B, S, MIX_W)


def diff_attention(q, k, v, lam, lam_init, subln_g):
    B, _, H, S, d = q.shape
    nb = S // Q_BLOCK
    qb = q.reshape(B, 2, H, nb, Q_BLOCK, d).transpose(3, 0, 1, 2, 4, 5)
    starts = jnp.arange(nb, dtype=jnp.int32) * Q_BLOCK
    kpos = jnp.arange(S, dtype=jnp.int32)
    scale = HEAD_DIM ** -0.5

    def block(args):
        qi, s0 = args
        s = jnp.einsum('bchqd,bchkd->bchqk', qi, k).astype(jnp.float32) * scale
        qpos = s0 + jnp.arange(Q_BLOCK, dtype=jnp.int32)
        mask = kpos[None, :] <= qpos[:, None]
        s = jnp.where(mask, s, -jnp.inf)
        p = jax.nn.softmax(s, axis=-1)
        a = p[:, 0] - lam * p[:, 1]
        return jnp.einsum('bhqk,bhkd->bhqd', a.astype(v.dtype), v)

    o = lax.map(block, (qb, starts))
    o = o.transpose(1, 0, 3, 2, 4).reshape(B, S, H, DIFF_VDIM)
    o = rmsnorm(o, subln_g) * (1.0 - lam_init)
    return o.reshape(B, S, MIX_W)


def mem_attention(qm, mk, mv):
    B, H, S, d = qm.shape
    s = jnp.einsum('bhqd,bhkd->bhqk', qm, mk).astype(jnp.float32) * (HEAD_DIM ** -0.5)
    p = jax.nn.softmax(s, axis=-1)
    o = jnp.einsum('bhqk,bhkd->bhqd', p.astype(mv.dtype), mv)
    return o.transpose(0, 2, 1, 3).reshape(B, S, MEM_W)


def setup_inputs(seed: int = 0) -> dict:
    key = jax.random.key(seed)
    ks = jax.random.split(key, 24)

    def nrm(k, shape, scale):
        return jax.random.normal(k, shape, jnp.float32) * scale

    def gain(k, shape):
        return 1.0 + 0.02 * jax.random.normal(k, shape, jnp.float32)

    return {
        "x": nrm(ks[0], (BATCH, SEQ, D_MODEL), 1.0),
        "mem": nrm(ks[1], (BATCH, MEM_LEN, D_MODEL), 1.0),
        "norm_g": gain(ks[2], (DEPTH, 3, D_MODEL)),
        "ffn_w_gu": nrm(ks[3], (DEPTH, 2, D_MODEL, 2 * D_FF), D_MODEL ** -0.5),
        "ffn_w_down": nrm(ks[4], (DEPTH, 2, D_FF, D_MODEL), D_FF ** -0.5),
        "w_out": nrm(ks[5], (DEPTH, D_MODEL, D_MODEL), D_MODEL ** -0.5),
        "mem_norm_g": gain(ks[6], (DEPTH, D_MODEL)),
        "mem_w_kv": nrm(ks[7], (DEPTH, D_MODEL, 2 * MEM_W), D_MODEL ** -0.5),
        "mem_q_norm_g": gain(ks[8], (DEPTH, HEAD_DIM)),
        "mem_k_norm_g": gain(ks[9], (DEPTH, HEAD_DIM)),
        "a_w_in": nrm(ks[10], (N_A, D_MODEL, 2 * MIX_W + MEM_W), D_MODEL ** -0.5),
        "a_v_norm_g": gain(ks[11], (N_A, MIX_W)),
        "a_w_s": nrm(ks[12], (N_A, GMLP_GROUPS, CHUNK, CHUNK), CHUNK ** -0.5),
        "a_b_s": gain(ks[13], (N_A, GMLP_GROUPS, CHUNK)),
        "kv_norm_g": gain(ks[14], (D_MODEL,)),
        "w_kv": nrm(ks[15], (D_MODEL, 2 * DIFF_HEADS * HEAD_DIM + DIFF_HEADS * DIFF_VDIM), D_MODEL ** -0.5),
        "k_norm_g": gain(ks[16], (HEAD_DIM,)),
        "b_w_in": nrm(ks[17], (N_B, D_MODEL, MIX_W + MEM_W), D_MODEL ** -0.5),
        "b_q_norm_g": gain(ks[18], (N_B, HEAD_DIM)),
        "b_lambda": nrm(ks[19], (N_B, 4, HEAD_DIM), 0.1),
        "b_subln_g": gain(ks[20], (N_B, DIFF_VDIM)),
    }


def reference(x, mem, norm_g, ffn_w_gu, ffn_w_down, w_out, mem_norm_g, mem_w_kv,
              mem_q_norm_g, mem_k_norm_g, a_w_in, a_v_norm_g, a_w_s, a_b_s,
              kv_norm_g, w_kv, k_norm_g, b_w_in, b_q_norm_g, b_lambda, b_subln_g):
    B, S, _ = x.shape
    M = mem.shape[1]
    cos, sin = rope_tables(S)
    HQK = DIFF_HEADS * HEAD_DIM
    k_sh = None
    v_sh = None
    for l in range(DEPTH):
        if l == N_A:
            kv = rmsnorm(x, kv_norm_g) @ w_kv
            k_sh = kv[..., :2 * HQK].reshape(B, S, 2, DIFF_HEADS, HEAD_DIM).transpose(0, 2, 3, 1, 4)
            k_sh = apply_rope(rmsnorm(k_sh, k_norm_g), cos, sin)
            v_sh = kv[..., 2 * HQK:].reshape(B, S, DIFF_HEADS, DIFF_VDIM).transpose(0, 2, 1, 3)

        x = x + 0.5 * swiglu(rmsnorm(x, norm_g[l, 0]), ffn_w_gu[l, 0], ffn_w_down[l, 0])

        mkv = rmsnorm(mem, mem_norm_g[l]) @ mem_w_kv[l]
        mk = mkv[..., :MEM_W].reshape(B, M, MEM_HEADS, HEAD_DIM).transpose(0, 2, 1, 3)
        mk = rmsnorm(mk, mem_k_norm_g[l])
        mv = mkv[..., MEM_W:].reshape(B, M, MEM_HEADS, HEAD_DIM).transpose(0, 2, 1, 3)

        h = rmsnorm(x, norm_g[l, 1])
        if l < N_A:
            z = h @ a_w_in[l]
            mix = gmlp_mix(jax.nn.gelu(z[..., :2 * MIX_W], approximate=False),
                           a_v_norm_g[l], a_w_s[l], a_b_s[l])
            qm = z[..., 2 * MIX_W:]
        else:
            j = l - N_A
            z = h @ b_w_in[j]
            q = z[..., :MIX_W].reshape(B, S, 2, DIFF_HEADS, HEAD_DIM).transpose(0, 2, 3, 1, 4)
            q = apply_rope(rmsnorm(q, b_q_norm_g[j]), cos, sin)
            lam_init = 0.8 - 0.6 * math.exp(-0.3 * l)
            lp = b_lambda[j].astype(jnp.float32)
            lam = jnp.exp(jnp.sum(lp[0] * lp[1])) - jnp.exp(jnp.sum(lp[2] * lp[3])) + lam_init
            mix = diff_attention(q, k_sh, v_sh, lam, lam_init, b_subln_g[j])
            qm = z[..., MIX_W:]
        qm = rmsnorm(qm.reshape(B, S, MEM_HEADS, HEAD_DIM), mem_q_norm_g[l]).transpose(0, 2, 1, 3)
        mo = mem_attention(qm, mk, mv)
        x = x + jnp.concatenate([mix, mo], axis=-1) @ w_out[l]

        x = x + 0.5 * swiglu(rmsnorm(x, norm_g[l, 2]), ffn_w_gu[l, 1], ffn_w_down[l, 1])
    return x
```

```python
import numpy as np
import math
from contextlib import ExitStack
import concourse.bass as bass
import concourse.mybir as mybir
from concourse.bass_utils import run_bass_kernel_spmd

F32 = mybir.dt.float32
BF16 = mybir.dt.bfloat16
AF = mybir.ActivationFunctionType
ALU = mybir.AluOpType
AX = mybir.AxisListType

D = 2048
NC_ = 16
DFF = 5632
NF = 44
TPC = 2048
SEQ = 4096
MIXW = 1536
MEMW = 512
MEML = 256
EPS = 1e-6
DEPTH = 4
N_A = 2
ENGS = ("pe", "act", "dve", "pool", "sp")
SLOT_EL = 5632
NSLOT = 4


class Buf:
    __slots__ = ("name", "w", "r")

    def __init__(self, name):
        self.name = name
        self.w = None
        self.r = {}


class DSem:
    def __init__(self, prog, name):
        self.h = prog.new_sem(name)
        self.count = 0


class Prog:
    def __init__(self, nc, stack, same_engine_sync=True):
        self.nc = nc
        self.stack = stack
        self.q = {e: [] for e in ENGS}
        self.S = {e: self.new_sem("S_" + e) for e in ENGS}
        self.cnt = {e: 0 for e in ENGS}
        self.seen = {e: {} for e in ENGS}
        self.same = same_engine_sync
        self.off = 16512

    def new_sem(self, name):
        return self.stack.enter_context(self.nc.semaphore(name))

    def sb(self, name, shape, dt):
        nbytes = int(np.prod(shape[1:])) * (2 if dt == BF16 else 4)
        nbytes = (nbytes + 63) // 64 * 64
        t = self.nc.alloc_sbuf_tensor_at(name, list(shape), dt, offset=self.off)
        self.off += nbytes
        assert self.off <= 229344, (name, self.off)
        return t

    def ps(self, name, shape, dt):
        return self.stack.enter_context(self.nc.psum_tensor(name, shape, dt))

    def _deps(self, eng, reads, writes):
        ev = {}

        def add(k, v):
            if ev.get(k, 0) < v:
                ev[k] = v
        for b in reads:
            if b.w is not None:
                add(*b.w)
        for b in writes:
            if b.w is not None:
                add(*b.w)
            for k, v in b.r.items():
                add(k, v)
        waits = []
        own = self.S[eng]
        for k, v in ev.items():
            if k is own and (eng == "pe" or not self.same):
                continue
            if self.seen[eng].get(k, 0) >= v:
                continue
            self.seen[eng][k] = v
            waits.append((k, v))
        return waits

    def _mark(self, event, reads, writes):
        k, v = event
        for b in reads:
            b.r[k] = v
        for b in writes:
            b.w = event
            b.r = {}

    def op(self, eng, fns, reads=(), writes=()):
        if callable(fns):
            fns = [fns]
        waits = self._deps(eng, reads, writes)
        self.cnt[eng] += 1
        event = (self.S[eng], self.cnt[eng])
        self._mark(event, reads, writes)
        self.q[eng].append((waits, fns, (self.S[eng], 1)))
        return event

    def dma(self, eng, dsem, out, in_, reads=(), writes=(), **kw):
        waits = self._deps(eng, reads, writes)
        dsem.count += 16
        event = (dsem.h, dsem.count)
        self._mark(event, reads, writes)

        def fn(e, out=out, in_=in_, kw=kw):
            return e.dma_start(out=out, in_=in_, **kw)
        self.q[eng].append((waits, [fn], (dsem.h, 16)))
        return event

    def collective(self, name, fn, reads=(), writes=()):
        h = self.new_sem(name)
        waits = self._deps("pool", reads, writes)
        event = (h, 1)
        self._mark(event, reads, writes)
        self.q["pool"].append((waits, [fn], (h, None)))
        return event

    def wait_all(self, eng, events):
        waits = []
        for k, v in events:
            if self.seen[eng].get(k, 0) < v:
                self.seen[eng][k] = v
                waits.append((k, v))
        self.q[eng].append((waits, [], None))

    def finish(self):
        nc = self.nc
        engobj = {"pe": "tensor", "act": "scalar", "dve": "vector", "pool": "gpsimd", "sp": "sync"}
        with nc.Block() as block:
            def make(ename):
                def body(e):
                    for waits, fns, inc in self.q[ename]:
                        for k, v in waits:
                            e.wait_ge(k, v)
                        last = None
                        for f in fns:
                            last = f(e)
                        if inc is not None and last is not None:
                            if inc[1] is None:
                                last.then_inc(inc[0])
                            else:
                                last.then_inc(inc[0], inc[1])
                return body
            for ename in ENGS:
                getattr(block, engobj[ename])(make(ename))


class WStream:
    def __init__(self, p, plan):
        self.p = p
        self.plan = plan
        self.slots = [p.sb("wslot%d" % i, [128, SLOT_EL], BF16) for i in range(NSLOT)]
        self.bufs = [Buf("wslot%d" % i) for i in range(NSLOT)]
        self.sems = [DSem(p, "wsem%d" % i) for i in range(NSLOT)]
        self.next_load = 0
        self.next_use = 0
        self.started = False

    def start(self):
        if self.started:
            return
        self.started = True
        for _ in range(min(NSLOT, len(self.plan))):
            self._load()

    def _load(self):
        i = self.next_load
        tag, pieces = self.plan[i]
        s = i % NSLOT
        for src, dstf in pieces:
            self.p.dma("pool", self.sems[s], dstf(self.slots[s]), src, writes=[self.bufs[s]])
        self.next_load += 1

    def acquire(self, tag):
        i = self.next_use
        assert self.plan[i][0] == tag, (self.plan[i][0], tag)
        s = i % NSLOT
        return self.slots[s], self.bufs[s]

    def release(self):
        self.next_use += 1
        if self.next_load < len(self.plan):
            self._load()


def I(name, **kw):
    return lambda e, name=name, kw=kw: getattr(e, name)(**kw)


VOFF = {}


def _layout():
    off = 0
    for nm, n in (("norm_g", DEPTH * 3 * NC_), ("mem_norm_g", DEPTH * NC_), ("kv_norm_g", NC_),
                  ("mem_q_norm_g", DEPTH), ("mem_k_norm_g", DEPTH), ("k_norm_g", 1),
                  ("b_q_norm_g", 2), ("b_subln_g", 4)):
        VOFF[nm] = off
        off += n
    return off


NVEC = _layout()


def pack_vecs(inp):
    v = np.zeros((128, NVEC), np.float32)

    def put(nm, arr):
        arr = np.asarray(arr, np.float32)
        v[:, VOFF[nm]:VOFF[nm] + arr.shape[0]] = arr.T
    put("norm_g", np.asarray(inp["norm_g"]).reshape(DEPTH * 3 * NC_, 128))
    put("mem_norm_g", np.asarray(inp["mem_norm_g"]).reshape(DEPTH * NC_, 128))
    put("kv_norm_g", np.asarray(inp["kv_norm_g"]).reshape(NC_, 128))
    put("mem_q_norm_g", np.asarray(inp["mem_q_norm_g"]).reshape(DEPTH, 128))
    put("mem_k_norm_g", np.asarray(inp["mem_k_norm_g"]).reshape(DEPTH, 128))
    put("k_norm_g", np.asarray(inp["k_norm_g"]).reshape(1, 128))
    put("b_q_norm_g", np.asarray(inp["b_q_norm_g"]).reshape(2, 128))
    put("b_subln_g", np.asarray(inp["b_subln_g"]).reshape(4, 128))
    return v


def make_consts():
    c = np.zeros((128, 3, 128), np.float32)
    s = np.arange(128)
    c[:, 0, :] = (s[None, :] >= s[:, None]).astype(np.float32)
    c[:, 1, :] = np.where(s[None, :] >= s[:, None], 0.0, -30000.0)
    for po in range(64):
        c[po + 64, 2, po] = -1.0
    for po in range(64, 128):
        c[po - 64, 2, po] = 1.0
    return c


def rope_tables_T(pos0, n):
    pos = np.arange(pos0, pos0 + n, dtype=np.float32)
    inv = (np.float32(10000.0) ** (-np.arange(0, 128, 2, dtype=np.float32) / np.float32(128))).astype(np.float32)
    ang = pos[:, None] * inv[None, :]
    ang = np.concatenate([ang, ang], axis=-1)
    return np.stack([np.cos(ang).T, np.sin(ang).T]).astype(np.float32)


class K:
    def __init__(self, cfg):
        self.cfg = cfg
        steps = cfg["steps"]
        nc = self.nc = bass.Bass("TRN2", target_bir_lowering=False)
        dt = nc.dram_tensor
        inp = lambda n, sh, d=F32: dt(n, sh, d, kind="ExternalInput").ap()
        self.xT_in = inp("xT", [D, TPC])
        self.oT = dt("oT", [D, TPC], F32, kind="ExternalOutput").ap()
        self.w_gu = inp("ffn_w_gu", [DEPTH, 2, D, 2 * DFF])
        self.w_dn = inp("ffn_w_down", [DEPTH, 2, DFF, D])
        self.w_out = inp("w_out", [DEPTH, D, D])
        self.mem_w_kv = inp("mem_w_kv", [DEPTH, D, 2 * MEMW])
        self.vecs = inp("vecs", [128, NVEC])
        self.consts = inp("consts", [128, 3, 128])
        self.memT = inp("memT", [D, MEML])
        self.csteps = [(s_ if (s_[0] in ("first", "own") and len(s_) == 2 and isinstance(s_[1], tuple)) else ("own", s_))
                       for s_ in steps]
        self.kinds = set(st_[0] for _, st_ in self.csteps)
        if "mixA" in self.kinds:
            self.a_w_in = inp("a_w_in", [N_A, D, 2 * MIXW + MEMW])
            self.a_vg = inp("a_vg_rep", [N_A, 128, MIXW])
            self.a_wsT = inp("a_wsT", [N_A, 128, 6, 128])
            self.a_bs = inp("a_bs", [N_A, 1, 6 * 128])
        if "kv" in self.kinds or "mixB" in self.kinds:
            self.ropeT = inp("ropeT", [2, 128, TPC])
        if "kv" in self.kinds:
            self.w_kv = inp("w_kv", [D, 3072])
        if "mixB" in self.kinds:
            self.b_w_in = inp("b_w_in", [2, D, D])
            self.lamrep = inp("lamrep", [2, 128, 4, 128])
            self.kbias = inp("kbias", [128, 32])
        mode = cfg.get("kvmode", "none")
        if mode == "out":
            self.KT_own = dt("KT_own", [12, 128, TPC], BF16, kind="ExternalOutput").ap()
            self.V_own = dt("V_own", [TPC, MIXW], BF16, kind="ExternalOutput").ap()
        elif mode == "fused":
            self.xT_first = inp("xT_first", [D, TPC])
            self.ropeT_first = inp("ropeT_first", [2, 128, TPC])
            self.fT = dt("fT", [D, TPC], F32).ap()
            self.KT_own = dt("KT_own", [12, 128, TPC], BF16).ap()
            self.V_own = dt("V_own", [TPC, MIXW], BF16).ap()
            self.KT_first = dt("KT_first", [12, 128, TPC], BF16).ap()
            self.V_first = dt("V_first", [TPC, MIXW], BF16).ap()
        elif mode == "xchg":
            self.KTo_t = [dt("KTo%d" % q_, [128, TPC], BF16) for q_ in range(12)]
            self.KTg_t = [dt("KTg%d" % q_, [256, TPC], BF16) for q_ in range(12)]
            self.Vo_t = [dt("Vo%d" % g_, [512, MIXW], BF16) for g_ in range(4)]
            self.Vg_t = [dt("Vg%d" % g_, [1024, MIXW], BF16) for g_ in range(4)]
            self.KT_own = [t_.ap() for t_ in self.KTo_t]
            self.KT_first = [t_.ap()[0:128, :] for t_ in self.KTg_t]
            self.V_own = [t_.ap() for t_ in self.Vo_t]
            self.V_first = [t_.ap()[0:512, :] for t_ in self.Vg_t]
        elif mode == "both":
            self.KT_own = dt("KT_own", [12, 128, TPC], BF16, kind="ExternalOutput").ap()
            self.V_own = dt("V_own", [TPC, MIXW], BF16, kind="ExternalOutput").ap()
            self.KT_first = inp("KT_first", [12, 128, TPC], BF16)
            self.V_first = inp("V_first", [TPC, MIXW], BF16)
        elif mode == "in":
            self.KT_own = inp("KT_own", [12, 128, TPC], BF16)
            self.V_own = inp("V_own", [TPC, MIXW], BF16)
            self.KT_first = inp("KT_first", [12, 128, TPC], BF16)
            self.V_first = inp("V_first", [TPC, MIXW], BF16)

    @staticmethod
    def _cols(w2d, c0, n):
        src = w2d.rearrange("(c p) n -> p c n", p=128)[:, :, c0:c0 + n]
        return [(src, lambda sl, n=n: sl[:, 0:NC_ * n].rearrange("p (c n) -> p c n", c=NC_))]

    def plan_ffn(self, L, which):
        pl = []
        for grp in range(2):
            for j in range(NF):
                pieces = []
                for two in range(2):
                    src = self.w_gu[L, which].rearrange("(c p) n -> p c n", p=128)[:, :, two * DFF + j * 128:two * DFF + (j + 1) * 128]
                    pieces.append((src, lambda sl, two=two: sl[:, 0:NC_ * 256].rearrange("p (c two f) -> p c two f", c=NC_, two=2)[:, :, two, :]))
                pl.append((("gu", L, which, grp, j), pieces))
            for dc in range(NC_):
                src = self.w_dn[L, which].rearrange("(c p) n -> p c n", p=128)[:, :, dc * 128:(dc + 1) * 128]
                pl.append((("dn", L, which, grp, dc), [(src, lambda sl: sl[:, 0:NF * 128].rearrange("p (f n) -> p f n", f=NF))]))
        return pl

    def plan_memkv(self, L):
        pl = []
        for h in range(4):
            pl.append((("mk", L, h), self._cols(self.mem_w_kv[L], h * 128, 128)))
        for cb in range(2):
            pl.append((("mv", L, cb), self._cols(self.mem_w_kv[L], MEMW + cb * 256, 256)))
        return pl

    def plan_tail(self, L, grp, w_in, qm0):
        pl = []
        for h in range(4):
            pl.append((("qm", L, grp, h), self._cols(w_in, qm0 + h * 128, 128)))
        for dc in range(NC_):
            pl.append((("wo", L, grp, dc), self._cols(self.w_out[L], dc * 128, 128)))
        return pl

    def plan_mixA(self, L):
        pl = self.plan_memkv(L)
        for grp in range(4):
            for fc in range(12):
                pl.append((("au", L, grp, fc), self._cols(self.a_w_in[L], fc * 128, 128)))
            for cb in range(6):
                pl.append((("av", L, grp, cb), self._cols(self.a_w_in[L], MIXW + cb * 256, 256)))
            pl += self.plan_tail(L, grp, self.a_w_in[L], 2 * MIXW)
        return pl

    def plan_mixB(self, L):
        pl = self.plan_memkv(L)
        for grp in range(4):
            for qc in range(12):
                pl.append((("bq", L, grp, qc), self._cols(self.b_w_in[L - N_A], qc * 128, 128)))
            pl += self.plan_tail(L, grp, self.b_w_in[L - N_A], MIXW)
        return pl

    def plan_kv(self):
        pl = []
        for grp in range(4):
            for qc in range(12):
                pl.append((("kk", grp, qc), self._cols(self.w_kv, qc * 128, 128)))
            for h in range(6):
                pl.append((("kvv", grp, h), self._cols(self.w_kv, MIXW + h * 256, 256)))
        return pl

    def build(self):
        nc, cfg = self.nc, self.cfg
        with ExitStack() as st:
            p = self.p = Prog(nc, st, same_engine_sync=cfg.get("same", True))
            B = self.B = {}

            def buf(n):
                B[n] = Buf(n)
                return B[n]
            self.vec_sb = p.sb("vecs", [128, NVEC], F32); buf("vecs")
            self.cst = p.sb("cst", [128, 3, 128], F32); buf("cst")
            self.ones_d = p.sb("ones_d", [128, 128], BF16)
            self.ones_h = p.sb("ones_h", [128, 128], BF16)
            self.ones_v = p.sb("ones_v", [128, 128], BF16)
            self.ones_1 = p.sb("ones_1", [128, 128], BF16)
            buf("ones")
            self.eps_col = p.sb("eps_col", [128, 1], F32); buf("eps")
            self.hT = p.sb("hT", [128, NC_, 1024], BF16)
            self.xs = p.sb("xs", [128, NC_, 256], F32); buf("xs")
            self.sq = p.sb("sq", [128, NC_, 256], BF16); buf("sq")
            self.rstd = p.sb("rstd", [128, 256], F32); buf("rstd")
            self.rtmp = p.sb("rtmp", [128, 512], F32); buf("rtmp")
            self.sg = [p.sb("sg%d" % i, [128, 512], F32) for i in range(2)]
            self.yst = [p.sb("yst%d" % i, [128, 512], F32) for i in range(2)]
            for i in range(2):
                buf("sg%d" % i); buf("yst%d" % i)
            self.scr0 = p.off
            self.actT = p.sb("actT", [128, NF, 1024], BF16)
            self.scr1 = p.off
            self.psb = [p.ps("ps%d" % i, [128, 512], F32) for i in range(8)]
            for i in range(8):
                buf("ps%d" % i)
            for c in range(4):
                buf("hT%d" % c)
            self.mixbuf = [Buf("mixT%d" % c) for c in range(NC_)]
            self.ffn_bufs = []
            for j in range(NF):
                for tb in range(2):
                    self.ffn_bufs.append(buf("act%d_%d" % (j, tb)))
            self.mix_bufs = []
            self.scr_owner = "ffn"
            self.ctxs = {}
            for cn in (["first", "own"] if cfg.get("kvmode") == "fused" else ["own"]):
                self.ctxs[cn] = dict(
                    xbuf=[Buf("xdram_%s%d" % (cn, g)) for g in range(4)],
                    xsem=[DSem(p, "xsem_%s%d" % (cn, g)) for g in range(4)],
                    kvbuf=Buf("kvdram_" + cn), kvsem=DSem(p, "kvsem_" + cn))
            self.res_own = self.oT
            if "first" in self.ctxs:
                self.firstK_bufs = [self.ctxs["first"]["kvbuf"]] * 12
                self.firstV_bufs = [self.ctxs["first"]["kvbuf"]] * 4
            else:
                self.firstK_bufs = [Buf("gK%d" % q_) for q_ in range(12)]
                self.firstV_bufs = [Buf("gV%d" % g_) for g_ in range(4)]
            self.ldsem = DSem(p, "ldsem")
            self.csem = DSem(p, "csem")
            self.isem = DSem(p, "isem")
            self._msems = {}
            self._dbg = {}
            self.rr = 0
            plan = []
            csteps = self.csteps
            for cn, stp in csteps:
                kind = stp[0]
                if kind == "ffn":
                    sub = self.plan_ffn(stp[1], stp[2])
                elif kind == "mixA":
                    sub = self.plan_mixA(stp[1])
                elif kind == "mixB":
                    sub = self.plan_mixB(stp[1])
                elif kind == "kv":
                    sub = self.plan_kv()
                elif kind == "xchg":
                    sub = []
                plan += [((cn,) + t, pcs) for t, pcs in sub]
            p.dma("sp", self.csem, self.vec_sb[:], self.vecs, writes=[B["vecs"]])
            p.dma("sp", self.isem, self.cst[:], self.consts, writes=[B["cst"]])
            p.op("dve", [I("memset", ap=self.ones_d[:], constant=1.0 / D),
                         I("memset", ap=self.ones_h[:], constant=1.0 / 128),
                         I("memset", ap=self.ones_v[:], constant=1.0 / 256),
                         I("memset", ap=self.ones_1[:], constant=1.0)], writes=[B["ones"]])
            p.op("dve", I("memset", ap=self.eps_col[:], constant=EPS), writes=[B["eps"]])
            for cn in self.ctxs:
                self.set_ctx(cn)
                srcx = self.xT_first if cn == "first" else self.xT_in
                for g in range(4):
                    p.dma("sp", self.xsem[g], self.oT[:, g * 512:(g + 1) * 512], srcx[:, g * 512:(g + 1) * 512],
                          writes=[self.xbuf[g]])
            self.ws = WStream(p, plan)
            self.ws.start()
            for cn, stp in csteps:
                self.set_ctx(cn)
                kind = stp[0]
                if kind == "ffn":
                    for grp in range(2):
                        self.ffn_group(stp[1], stp[2], grp)
                elif kind == "mixA":
                    self.mixA(stp[1])
                elif kind == "mixB":
                    self.mixB(stp[1])
                elif kind == "kv":
                    self.kvproj()
                elif kind == "xchg":
                    self.exchange()
            assert self.ws.next_use == len(plan), (self.ws.next_use, len(plan))
            evs = list(self._dbg.values())
            for c_ in self.ctxs.values():
                evs += [b.w for b in c_["xbuf"] if b.w is not None]
                if c_["kvbuf"].w is not None:
                    evs.append(c_["kvbuf"].w)
            p.wait_all("pool", evs)
            p.wait_all("sp", evs)
            p.finish()
        return nc

    def scratch(self, owner, newbufs):
        if self.scr_owner == owner and owner == "ffn":
            return
        hx = [self.B["hT2"], self.B["hT3"]]
        old = (self.ffn_bufs + hx) if self.scr_owner == "ffn" else self.mix_bufs
        if owner == "ffn":
            newbufs = list(newbufs) + hx
        merged = {}
        for b in old:
            if b.w is not None and merged.get(b.w[0], 0) < b.w[1]:
                merged[b.w[0]] = b.w[1]
            for k, v in b.r.items():
                if merged.get(k, 0) < v:
                    merged[k] = v
        for b in newbufs:
            b.w = None
            b.r = dict(merged)
        self.scr_owner = owner

    def msem(self, name):
        if name not in self._msems:
            self._msems[name] = DSem(self.p, "ms_" + name)
        return self._msems[name]

    def dbg(self, name, ap, bufs, shape, dt_):
        if not self.cfg.get("debug") or name in self._dbg:
            return
        d = self.nc.dram_tensor("dbg_" + name, list(shape), dt_, kind="ExternalOutput").ap()
        self._dbg[name] = self.p.dma("sp", self.msem("dbg_" + name), d, ap, reads=bufs)

    def bank(self, pool=None):
        pool = pool or (0, 1, 2, 3, 4, 5, 6, 7)
        self.rr += 1
        i = pool[self.rr % len(pool)]
        return self.psb[i], self.B["ps%d" % i]

    def rsqrt(self, src, srcbuf, dst, dstbuf, n, post=None, scale=1.0):
        p, B = self.p, self.B
        p.op("act", I("activation", out=self.rtmp[:, 0:n], in_=src, func=AF.Sqrt, bias=self.eps_col[:, 0:1], scale=scale),
             reads=[srcbuf, B["eps"]], writes=[B["rtmp"]])
        p.op("dve", I("reciprocal", out=dst, in_=self.rtmp[:, 0:n]), reads=[B["rtmp"]], writes=[dstbuf])
        if post is not None:
            p.op("dve", I("tensor_scalar", out=dst, in0=dst, scalar1=post, scalar2=None, op0=ALU.mult), reads=[dstbuf], writes=[dstbuf])

    def norm(self, src2d, srcbufs, gcol, tok0, ntok):
        p, B = self.p, self.B
        src3 = src2d.rearrange("(c p) t -> p c t", p=128)
        for sb_ in range(ntok // 256):
            t0 = tok0 + sb_ * 256
            p.dma("sp", self.ldsem, self.xs[:], src3[:, :, t0:t0 + 256], reads=[srcbufs(t0)], writes=[B["xs"]])
            p.op("act", I("activation", out=self.sq[:], in_=self.xs[:], func=AF.Square), reads=[B["xs"]], writes=[B["sq"]])
            ps, psb = self.bank()
            fns = [I("matmul", out=ps[:, 0:256], lhsT=self.ones_d[:], rhs=self.sq[:, c, :], start=(c == 0), stop=(c == NC_ - 1))
                   for c in range(NC_)]
            p.op("pe", fns, reads=[B["sq"], B["ones"]], writes=[psb])
            self.rsqrt(ps[:, 0:256], psb, self.rstd[:], B["rstd"], 256)
            l0 = sb_ * 256
            fns = [I("scalar_tensor_tensor", out=self.hT[:, c, l0:l0 + 256], in0=self.xs[:, c, :],
                     scalar=self.vec_sb[:, gcol + c:gcol + c + 1], in1=self.rstd[:], op0=ALU.mult, op1=ALU.mult)
                   for c in range(NC_)]
            p.op("dve", fns, reads=[B["xs"], B["rstd"], B["vecs"]], writes=[B["hT%d" % sb_]])

    def set_ctx(self, cn):
        c = self.ctxs[cn]
        self.ctx = cn
        self.xbuf, self.xsem, self.kvbuf, self.kvsem = c["xbuf"], c["xsem"], c["kvbuf"], c["kvsem"]
        fused = self.cfg.get("kvmode") == "fused"
        if cn == "first":
            self.oT, self.KT_w, self.V_w, self.rope_src = self.fT, self.KT_first, self.V_first, self.ropeT_first
        else:
            self.oT = self.res_own
            self.KT_w, self.V_w = getattr(self, "KT_own", None), getattr(self, "V_own", None)
            self.rope_src = getattr(self, "ropeT", None)

    @staticmethod
    def vchunks(v):
        if isinstance(v, list):
            return v
        return [v[g_ * 512:(g_ + 1) * 512, :] for g_ in range(4)]

    def exchange(self):
        p = self.p
        pairs = [[0, 1], [2, 3], [4, 5], [6, 7]]
        for q_ in range(12):
            fn = (lambda e, a=self.KTo_t[q_], b=self.KTg_t[q_]: e.collective_compute(
                "AllGather", ALU.bypass, replica_groups=pairs, ins=[a.ap().opt()], outs=[b.ap().opt()]))
            p.collective("ccK%d" % q_, fn, reads=[self.kvbuf], writes=[self.firstK_bufs[q_]])
        for g_ in range(4):
            fn = (lambda e, a=self.Vo_t[g_], b=self.Vg_t[g_]: e.collective_compute(
                "AllGather", ALU.bypass, replica_groups=pairs, ins=[a.ap().opt()], outs=[b.ap().opt()]))
            p.collective("ccV%d" % g_, fn, reads=[self.kvbuf], writes=[self.firstV_bufs[g_]])

    def xsrc(self, t0):
        return self.xbuf[t0 // 512]

    def ffn_group(self, L, which, grp):
        p, B, ws = self.p, self.B, self.ws
        self.scratch("ffn", self.ffn_bufs)
        tok0 = grp * 1024
        gcol = VOFF["norm_g"] + (L * 3 + (0 if which == 0 else 2)) * NC_
        self.norm(self.oT, self.xsrc, gcol, tok0, 1024)
        hbufs = [B["hT%d" % c] for c in range(4)]
        for j in range(NF):
            slot, sbuf = ws.acquire((self.ctx, "gu", L, which, grp, j))
            w = slot[:, 0:NC_ * 256].rearrange("p (c two f) -> p c two f", c=NC_, two=2)
            for tb in range(2):
                base = (j % 2) * 4 + tb * 2
                psg, psu = self.psb[base], self.psb[base + 1]
                fns = []
                for half, pst in ((0, psg), (1, psu)):
                    for c in range(NC_):
                        fns.append(I("matmul", out=pst[:], lhsT=w[:, c, half, :], rhs=self.hT[:, c, tb * 512:(tb + 1) * 512],
                                     start=(c == 0), stop=(c == NC_ - 1)))
                p.op("pe", fns, reads=[sbuf, hbufs[tb * 2], hbufs[tb * 2 + 1]],
                     writes=[B["ps%d" % base], B["ps%d" % (base + 1)]])
                p.op("act", I("activation", out=self.sg[tb][:], in_=psg[:], func=AF.Silu),
                     reads=[B["ps%d" % base]], writes=[B["sg%d" % tb]])
                p.op("dve", I("tensor_tensor", out=self.actT[:, j, tb * 512:(tb + 1) * 512], in0=self.sg[tb][:], in1=psu[:], op=ALU.mult),
                     reads=[B["sg%d" % tb], B["ps%d" % (base + 1)]], writes=[B["act%d_%d" % (j, tb)]])
            ws.release()
        abufs = [[B["act%d_%d" % (j, tb)] for j in range(NF)] for tb in range(2)]
        for dc in range(NC_):
            slot, sbuf = ws.acquire((self.ctx, "dn", L, which, grp, dc))
            w = slot[:, 0:NF * 128].rearrange("p (f n) -> p f n", f=NF)
            for tb in range(2):
                bank = (dc * 2 + tb) % 8
                psy = self.psb[bank]
                fns = [I("matmul", out=psy[:], lhsT=w[:, f, :], rhs=self.actT[:, f, tb * 512:(tb + 1) * 512],
                         start=(f == 0), stop=(f == NF - 1)) for f in range(NF)]
                p.op("pe", fns, reads=[sbuf] + abufs[tb], writes=[B["ps%d" % bank]])
                yi = (dc * 2 + tb) % 2
                p.op("act", I("activation", out=self.yst[yi][:], in_=psy[:], func=AF.Copy, scale=0.5),
                     reads=[B["ps%d" % bank]], writes=[B["yst%d" % yi]])
                self.acc_x((tok0 + tb * 512) // 512, dc, self.yst[yi], B["yst%d" % yi])
            ws.release()

    def acc_x(self, g, dc, src, srcbuf):
        p = self.p
        xb = self.xbuf[g]
        tmp = Buf("tmp")
        tmp.r = dict(xb.r)
        tmp.w = xb.w if (xb.w is not None and xb.w[0] is not self.xsem[g].h) else None
        ev = p.dma("pool", self.xsem[g], self.oT[dc * 128:(dc + 1) * 128, g * 512:(g + 1) * 512], src[:],
                   reads=[srcbuf], writes=[tmp], accum_op=ALU.add)
        xb.w = ev
        xb.r = {}

    def mix_alloc(self, specs):
        p = self.p
        save = p.off
        p.off = self.scr0
        T = {}
        self.mix_bufs = []
        for name, shape, dt_ in specs:
            T[name] = p.sb("mx_%s_%d" % (name, self.rr), shape, dt_)
            assert p.off <= self.scr1, (name, p.off, self.scr1)
            b = Buf(name)
            self.B["m_" + name] = b
            self.mix_bufs.append(b)
            self.rr += 1
        p.off = save
        self.mix_bufs += self.mixbuf
        self.T = T
        self.scratch("mix", self.mix_bufs)
        return T

    COMMON = [("mkT", [128, 4, 256], BF16), ("mv", [128, 2, 512], BF16),
              ("qf", [128, 512], F32), ("qsq", [128, 512], BF16), ("hrl", [128, 512], F32),
              ("qnT", [128, 4, 512], BF16), ("pT0", [128, 2, 512], BF16), ("pT1", [128, 2, 512], BF16),
              ("rl", [128, 512], F32)]

    def mixT(self, c):
        return self.hT[:, c, 512:1024]

    def headnorm(self, ps, psbuf, n, gcol, out, outbuf):
        p, B, T = self.p, self.B, self.T
        p.op("act", [I("activation", out=T["qf"][:, 0:n], in_=ps[:, 0:n], func=AF.Copy),
                     I("activation", out=T["qsq"][:, 0:n], in_=ps[:, 0:n], func=AF.Square)],
             reads=[psbuf], writes=[B["m_qf"], B["m_qsq"]])
        ps2, ps2b = self.bank(self.free_banks)
        p.op("pe", I("matmul", out=ps2[:, 0:n], lhsT=self.ones_h[:], rhs=T["qsq"][:, 0:n], start=True, stop=True),
             reads=[B["m_qsq"], B["ones"]], writes=[ps2b])
        self.rsqrt(ps2[:, 0:n], ps2b, T["hrl"][:, 0:n], B["m_hrl"], n)
        p.op("dve", I("scalar_tensor_tensor", out=out, in0=T["qf"][:, 0:n], scalar=self.vec_sb[:, gcol:gcol + 1],
                      in1=T["hrl"][:, 0:n], op0=ALU.mult, op1=ALU.mult),
             reads=[B["m_qf"], B["m_hrl"], B["vecs"]], writes=[outbuf])

    def memkv(self, L):
        p, B, T, ws = self.p, self.B, self.T, self.ws
        memb = Buf("memT")
        self.norm(self.memT, lambda t0: memb, VOFF["mem_norm_g"] + L * NC_, 0, 256)
        hb = B["hT0"]
        for h in range(4):
            slot, sbuf = ws.acquire((self.ctx, "mk", L, h))
            w = slot[:, 0:NC_ * 128].rearrange("p (c n) -> p c n", c=NC_)
            ps, psb = self.bank(self.free_banks)
            p.op("pe", [I("matmul", out=ps[:, 0:256], lhsT=w[:, c, :], rhs=self.hT[:, c, 0:256], start=(c == 0), stop=(c == NC_ - 1))
                        for c in range(NC_)], reads=[sbuf, hb], writes=[psb])
            ws.release()
            self.headnorm(ps, psb, 256, VOFF["mem_k_norm_g"] + L, T["mkT"][:, h, :], B["m_mkT"])
        for cb in range(2):
            slot, sbuf = ws.acquire((self.ctx, "mv", L, cb))
            w = slot[:, 0:NC_ * 256].rearrange("p (c n) -> p c n", c=NC_)
            for mc in range(2):
                ps, psb = self.bank(self.free_banks)
                p.op("pe", [I("matmul", out=ps[:, 0:256], lhsT=self.hT[:, c, mc * 128:(mc + 1) * 128], rhs=w[:, c, :],
                              start=(c == 0), stop=(c == NC_ - 1)) for c in range(NC_)], reads=[sbuf, hb], writes=[psb])
                p.op("act", I("activation", out=T["mv"][:, mc, cb * 256:(cb + 1) * 256], in_=ps[:, 0:256], func=AF.Copy),
                     reads=[psb], writes=[B["m_mv"]])
            ws.release()

    def tail(self, L, grp):
        p, B, T, ws = self.p, self.B, self.T, self.ws
        hb = [B["hT0"], B["hT1"]]
        sc = 128.0 ** -0.5
        for h in range(4):
            slot, sbuf = ws.acquire((self.ctx, "qm", L, grp, h))
            w = slot[:, 0:NC_ * 128].rearrange("p (c n) -> p c n", c=NC_)
            ps, psb = self.bank(self.free_banks)
            p.op("pe", [I("matmul", out=ps[:], lhsT=w[:, c, :], rhs=self.hT[:, c, 0:512], start=(c == 0), stop=(c == NC_ - 1))
                        for c in range(NC_)], reads=[sbuf] + hb, writes=[psb])
            ws.release()
            self.headnorm(ps, psb, 512, VOFF["mem_q_norm_g"] + L, T["qnT"][:, h, :], B["m_qnT"])
            pT = T["pT%d" % (h % 2)]
            pTb = B["m_pT%d" % (h % 2)]
            for mc in range(2):
                ps, psb = self.bank(self.free_banks)
                p.op("pe", I("matmul", out=ps[:], lhsT=T["mkT"][:, h, mc * 128:(mc + 1) * 128], rhs=T["qnT"][:, h, :], start=True, stop=True),
                     reads=[B["m_mkT"], B["m_qnT"]], writes=[psb])
                p.op("act", I("activation", out=pT[:, mc, :], in_=ps[:], func=AF.Exp, scale=sc), reads=[psb], writes=[pTb])
            pso, psob = self.bank(self.free_banks)
            psl, pslb = self.bank(self.free_banks)
            p.op("pe", [I("matmul", out=pso[:], lhsT=T["mv"][:, mc, h * 128:(h + 1) * 128], rhs=pT[:, mc, :], start=(mc == 0), stop=(mc == 1))
                        for mc in range(2)] +
                       [I("matmul", out=psl[:], lhsT=self.ones_1[:], rhs=pT[:, mc, :], start=(mc == 0), stop=(mc == 1))
                        for mc in range(2)], reads=[B["m_mv"], pTb, B["ones"]], writes=[psob, pslb])
            p.op("dve", I("reciprocal", out=T["rl"][:], in_=psl[:]), reads=[pslb], writes=[B["m_rl"]])
            p.op("dve", I("tensor_tensor", out=self.mixT(12 + h), in0=pso[:], in1=T["rl"][:], op=ALU.mult),
                 reads=[psob, B["m_rl"]], writes=[self.mixbuf[12 + h]])
        self.dbg("qnT", T["qnT"][:].rearrange("p a b -> p (a b)"), [B["m_qnT"]], [128, 4 * 512], BF16)
        for c in range(NC_):
            self.dbg("mixT%d" % c, self.mixT(c), [self.mixbuf[c]], [128, 512], BF16)
        for dc in range(NC_):
            slot, sbuf = ws.acquire((self.ctx, "wo", L, grp, dc))
            w = slot[:, 0:NC_ * 128].rearrange("p (c n) -> p c n", c=NC_)
            ps, psb = self.bank(self.free_banks)
            p.op("pe", [I("matmul", out=ps[:], lhsT=w[:, c, :], rhs=self.mixT(c), start=(c == 0), stop=(c == NC_ - 1))
                        for c in range(NC_)], reads=[sbuf] + self.mixbuf, writes=[psb])
            ws.release()
            yi = dc % 2
            p.op("act", I("activation", out=self.yst[yi][:], in_=ps[:], func=AF.Copy), reads=[psb], writes=[B["yst%d" % yi]])
            self.acc_x(grp, dc, self.yst[yi], B["yst%d" % yi])

    def mixA(self, L):
        p, B, ws = self.p, self.B, self.ws
        self.free_banks = (0, 1, 2, 3, 4, 5, 6, 7)
        T = self.mix_alloc(self.COMMON + [
            ("wsf", [128, 6, 128], F32), ("wsT", [128, 6, 128], BF16), ("bsf", [1, 768], F32), ("bsr", [1, 768], BF16),
            ("vg", [128, MIXW], F32), ("uT", [128, 12, 512], BF16), ("v", [128, 4, MIXW], F32), ("vn", [128, 4, MIXW], BF16),
            ("ssq", [128, 24], F32), ("rsv", [128, 4], F32), ("junk", [128, 256], BF16)])
        a = L
        p.dma("sp", self.msem("wsf"), T["wsf"][:], self.a_wsT[a], writes=[B["m_wsf"]])
        p.dma("sp", self.msem("bsf"), T["bsf"][:], self.a_bs[a], writes=[B["m_bsf"]])
        p.dma("sp", self.msem("vg"), T["vg"][:], self.a_vg[a], writes=[B["m_vg"]])
        p.op("dve", [I("tensor_tensor", out=T["wsT"][:, g, :], in0=T["wsf"][:, g, :], in1=self.cst[:, 0, :], op=ALU.mult) for g in range(6)]
             + [I("tensor_copy", out=T["bsr"][:], in_=T["bsf"][:])],
             reads=[B["m_wsf"], B["m_bsf"], B["cst"]], writes=[B["m_wsT"], B["m_bsr"]])
        self.memkv(L)
        gcol = VOFF["norm_g"] + (L * 3 + 1) * NC_
        hb = [B["hT0"], B["hT1"]]
        for grp in range(4):
            self.norm(self.oT, self.xsrc, gcol, grp * 512, 512)
            for fc in range(12):
                slot, sbuf = ws.acquire((self.ctx, "au", L, grp, fc))
                w = slot[:, 0:NC_ * 128].rearrange("p (c n) -> p c n", c=NC_)
                ps, psb = self.bank()
                p.op("pe", [I("matmul", out=ps[:], lhsT=w[:, c, :], rhs=self.hT[:, c, 0:512], start=(c == 0), stop=(c == NC_ - 1))
                            for c in range(NC_)], reads=[sbuf] + hb, writes=[psb])
                ws.release()
                p.op("act", I("activation", out=T["uT"][:, fc, :], in_=ps[:], func=AF.Gelu), reads=[psb], writes=[B["m_uT"]])
            self.vtb = [Buf("vt%d" % i) for i in range(24)]
            for b_ in self.vtb:
                b_.r = dict(B["m_v"].r)
                if B["m_v"].w is not None:
                    b_.r[B["m_v"].w[0]] = max(b_.r.get(B["m_v"].w[0], 0), B["m_v"].w[1])
            p.op("dve", I("memset", ap=T["ssq"][:], constant=0.0), writes=[B["m_ssq"]])
            for cb in range(6):
                slot, sbuf = ws.acquire((self.ctx, "av", L, grp, cb))
                w = slot[:, 0:NC_ * 256].rearrange("p (c n) -> p c n", c=NC_)
                for t in range(4):
                    ps, psb = self.bank()
                    p.op("pe", [I("matmul", out=ps[:, 0:256], lhsT=self.hT[:, c, t * 128:(t + 1) * 128], rhs=w[:, c, :],
                                  start=(c == 0), stop=(c == NC_ - 1)) for c in range(NC_)], reads=[sbuf] + hb, writes=[psb])
                    vsl = T["v"][:, t, cb * 256:(cb + 1) * 256]
                    vb_ = self.vtb[t * 6 + cb]
                    p.op("act", I("activation", out=vsl, in_=ps[:, 0:256], func=AF.Gelu), reads=[psb], writes=[vb_])
                    p.op("act", I("activation", out=T["junk"][:], in_=vsl, func=AF.Square,
                                  accum_out=T["ssq"][:, t * 6 + cb:t * 6 + cb + 1]),
                         reads=[vb_, B["m_ssq"]], writes=[B["m_junk"], B["m_ssq"]])
                ws.release()
            p.op("dve", I("tensor_reduce", out=T["rsv"][:], in_=T["ssq"][:].rearrange("p (t c) -> p t c", t=4), axis=AX.X, op=ALU.add),
                 reads=[B["m_ssq"]], writes=[B["m_rsv"]])
            self.dbg("rsv0", T["rsv"][:], [B["m_rsv"]], [128, 4], F32)
            self.rsqrt(T["rsv"][:], B["m_rsv"], T["rsv"][:], B["m_rsv"], 4, scale=1.0 / MIXW)
            p.op("dve", [I("scalar_tensor_tensor", out=T["vn"][:, t, :], in0=T["v"][:, t, :], scalar=T["rsv"][:, t:t + 1],
                           in1=T["vg"][:], op0=ALU.mult, op1=ALU.mult) for t in range(4)],
                 reads=[B["m_v"], B["m_rsv"], B["m_vg"]] + self.vtb, writes=[B["m_vn"], B["m_v"]])
            self.dbg("uT", T["uT"][:].rearrange("p a b -> p (a b)"), [B["m_uT"]], [128, 12 * 512], BF16)
            self.dbg("v", T["v"][:].rearrange("p a b -> p (a b)"), [B["m_v"]], [128, 4 * MIXW], F32)
            self.dbg("ssq", T["ssq"][:], [B["m_ssq"]], [128, 24], F32)
            self.dbg("rsv", T["rsv"][:], [B["m_rsv"]], [128, 4], F32)
            self.dbg("vn", T["vn"][:].rearrange("p a b -> p (a b)"), [B["m_vn"]], [128, 4 * MIXW], BF16)
            self.dbg("mkT", T["mkT"][:].rearrange("p a b -> p (a b)"), [B["m_mkT"]], [128, 4 * 256], BF16)
            self.dbg("mv", T["mv"][:].rearrange("p a b -> p (a b)"), [B["m_mv"]], [128, 2 * 512], BF16)
            for cc in range(12):
                g = cc // 2
                ps, psb = self.bank()
                fns = []
                for t in range(4):
                    fns.append(I("matmul", out=ps[:, t * 128:(t + 1) * 128], lhsT=T["vn"][:, t, cc * 128:(cc + 1) * 128],
                                 rhs=T["wsT"][:, g, :], start=True, stop=False))
                    fns.append(I("matmul", out=ps[:, t * 128:(t + 1) * 128], lhsT=self.ones_1[0:1, :],
                                 rhs=T["bsr"][0:1, g * 128:(g + 1) * 128], start=False, stop=True))
                p.op("pe", fns, reads=[B["m_vn"], B["m_wsT"], B["m_bsr"], B["ones"]], writes=[psb])
                p.op("dve", I("tensor_tensor", out=self.mixT(cc), in0=T["uT"][:, cc, :], in1=ps[:], op=ALU.mult),
                     reads=[B["m_uT"], psb], writes=[self.mixbuf[cc]])
            self.tail(L, grp)

    def rope(self, src, srcbuf, out, outbuf):
        p, B, T = self.p, self.B, self.T
        ps, psb = self.bank(self.free_banks)
        p.op("pe", I("matmul", out=ps[:], lhsT=self.cst[:, 2, :], rhs=src, start=True, stop=True),
             reads=[srcbuf, B["cst"]], writes=[psb])
        p.op("dve", [I("tensor_tensor", out=T["t1"][:], in0=src, in1=T["cos"][:], op=ALU.mult),
                     I("tensor_tensor", out=T["t2"][:], in0=ps[:], in1=T["sin"][:], op=ALU.mult)],
             reads=[srcbuf, psb, B["m_cos"], B["m_sin"]], writes=[B["m_t1"], B["m_t2"]])
        p.op("dve", I("tensor_tensor", out=out, in0=T["t1"][:], in1=T["t2"][:], op=ALU.add),
             reads=[B["m_t1"], B["m_t2"]], writes=[outbuf])

    def load_rope(self, grp):
        p, B, T = self.p, self.B, self.T
        p.dma("sp", self.msem("cos"), T["cos"][:], self.rope_src[0][:, grp * 512:(grp + 1) * 512], writes=[B["m_cos"]])
        p.dma("sp", self.msem("sin"), T["sin"][:], self.rope_src[1][:, grp * 512:(grp + 1) * 512], writes=[B["m_sin"]])

    ROPE = [("cos", [128, 512], F32), ("sin", [128, 512], F32), ("t1", [128, 512], F32), ("t2", [128, 512], F32)]

    def kvproj(self):
        p, B, ws = self.p, self.B, self.ws
        self.free_banks = (0, 1, 2, 3, 4, 5, 6, 7)
        T = self.mix_alloc(self.COMMON + self.ROPE + [("kst0", [128, 512], BF16), ("kst1", [128, 512], BF16),
                                                      ("vst0", [128, 256], BF16), ("vst1", [128, 256], BF16)])
        gcol = VOFF["kv_norm_g"]
        hb = [B["hT0"], B["hT1"]]
        for grp in range(4):
            self.norm(self.oT, self.xsrc, gcol, grp * 512, 512)
            self.load_rope(grp)
            for qc in range(12):
                slot, sbuf = ws.acquire((self.ctx, "kk", grp, qc))
                w = slot[:, 0:NC_ * 128].rearrange("p (c n) -> p c n", c=NC_)
                ps, psb = self.bank()
                p.op("pe", [I("matmul", out=ps[:], lhsT=w[:, c, :], rhs=self.hT[:, c, 0:512], start=(c == 0), stop=(c == NC_ - 1))
                            for c in range(NC_)], reads=[sbuf] + hb, writes=[psb])
                ws.release()
                self.headnorm(ps, psb, 512, VOFF["k_norm_g"], T["rl"][:], B["m_rl"])
                ks, ksb = T["kst%d" % (qc % 2)], B["m_kst%d" % (qc % 2)]
                self.rope(T["rl"][:], B["m_rl"], ks[:], ksb)
                self.kvbuf.w = p.dma("sp", self.kvsem, self.KT_w[qc][:, grp * 512:(grp + 1) * 512], ks[:], reads=[ksb])
            for h in range(6):
                slot, sbuf = ws.acquire((self.ctx, "kvv", grp, h))
                w = slot[:, 0:NC_ * 256].rearrange("p (c n) -> p c n", c=NC_)
                for t in range(4):
                    ps, psb = self.bank()
                    p.op("pe", [I("matmul", out=ps[:, 0:256], lhsT=self.hT[:, c, t * 128:(t + 1) * 128], rhs=w[:, c, :],
                                  start=(c == 0), stop=(c == NC_ - 1)) for c in range(NC_)], reads=[sbuf] + hb, writes=[psb])
                    vs, vsb = T["vst%d" % (t % 2)], B["m_vst%d" % (t % 2)]
                    p.op("act", I("activation", out=vs[:], in_=ps[:, 0:256], func=AF.Copy), reads=[psb], writes=[vsb])
                    r0 = grp * 512 + t * 128
                    self.kvbuf.w = p.dma("sp", self.kvsem, self.vchunks(self.V_w)[grp][t * 128:(t + 1) * 128, h * 256:(h + 1) * 256], vs[:], reads=[vsb])
                ws.release()

    def mixB(self, L):
        p, B, ws = self.p, self.B, self.ws
        j = L - N_A
        lam_init = 0.8 - 0.6 * math.exp(-0.3 * L)
        self.free_banks = (6, 7)
        T = self.mix_alloc(self.COMMON + self.ROPE + [
            ("QT", [128, 12, 512], BF16), ("kt", [128, 2, SEQ], BF16), ("vb", [128, 32, 256], BF16),
            ("pa0", [128, 512], BF16), ("pa1", [128, 512], BF16), ("pa2", [128, 512], BF16),
            ("mt0", [128, 128], F32), ("mt1", [128, 128], F32),
            ("ofp", [128, 2, 512], F32), ("rl0", [128, 512], F32), ("rl1", [128, 512], F32),
            ("osq", [128, 2, 512], BF16), ("lam", [128, 4, 128], F32), ("lsc", [128, 8], F32), ("kb", [128, 32], F32)])
        p.dma("sp", self.msem("lam"), T["lam"][:], self.lamrep[j], writes=[B["m_lam"]])
        p.dma("sp", self.msem("kb"), T["kb"][:], self.kbias, writes=[B["m_kb"]])
        lsc = T["lsc"]
        p.op("dve", [I("tensor_tensor", out=T["lam"][:, 0, :], in0=T["lam"][:, 0, :], in1=T["lam"][:, 1, :], op=ALU.mult),
                     I("tensor_tensor", out=T["lam"][:, 2, :], in0=T["lam"][:, 2, :], in1=T["lam"][:, 3, :], op=ALU.mult)],
             reads=[B["m_lam"]], writes=[B["m_lam"]])
        p.op("dve", [I("tensor_reduce", out=lsc[:, 0:1], in_=T["lam"][:, 0, :], axis=AX.X, op=ALU.add),
                     I("tensor_reduce", out=lsc[:, 1:2], in_=T["lam"][:, 2, :], axis=AX.X, op=ALU.add)],
             reads=[B["m_lam"]], writes=[B["m_lsc"]])
        p.op("act", I("activation", out=lsc[:, 2:4], in_=lsc[:, 0:2], func=AF.Exp), reads=[B["m_lsc"]], writes=[B["m_lsc"]])
        p.op("dve", I("tensor_tensor", out=lsc[:, 4:5], in0=lsc[:, 3:4], in1=lsc[:, 2:3], op=ALU.subtract),
             reads=[B["m_lsc"]], writes=[B["m_lsc"]])
        p.op("dve", I("tensor_scalar", out=lsc[:, 5:6], in0=lsc[:, 4:5], scalar1=-lam_init, scalar2=None, op0=ALU.add),
             reads=[B["m_lsc"]], writes=[B["m_lsc"]])
        neglam = lsc[:, 5:6]
        self.memkv(L)
        gcol = VOFF["norm_g"] + (L * 3 + 1) * NC_
        hb = [B["hT0"], B["hT1"]]
        sc = 128.0 ** -0.5
        pas = [(T["pa%d" % i], B["m_pa%d" % i]) for i in range(3)]
        pai = 0
        for grp in range(4):
            self.norm(self.oT, self.xsrc, gcol, grp * 512, 512)
            self.load_rope(grp)
            for qc in range(12):
                slot, sbuf = ws.acquire((self.ctx, "bq", L, grp, qc))
                w = slot[:, 0:NC_ * 128].rearrange("p (c n) -> p c n", c=NC_)
                ps, psb = self.bank(self.free_banks)
                p.op("pe", [I("matmul", out=ps[:], lhsT=w[:, c, :], rhs=self.hT[:, c, 0:512], start=(c == 0), stop=(c == NC_ - 1))
                            for c in range(NC_)], reads=[sbuf] + hb, writes=[psb])
                ws.release()
                self.headnorm(ps, psb, 512, VOFF["b_q_norm_g"] + j, T["rl"][:], B["m_rl"])
                self.rope(T["rl"][:], B["m_rl"], T["QT"][:, qc, :], B["m_QT"])
            nown = (grp + 1) * 512
            nkt = 16 + (grp + 1) * 4
            for h in range(6):
                for m in range(2):
                    qc = m * 6 + h
                    p.dma("sp", self.msem("kt"), T["kt"][:, m, 0:TPC], self.KT_first[qc], reads=[self.firstK_bufs[qc]], writes=[B["m_kt"]])
                    p.dma("sp", self.msem("kt"), T["kt"][:, m, TPC:TPC + nown], self.KT_own[qc][:, 0:nown], reads=[self.kvbuf], writes=[B["m_kt"]])
                vf, vo = self.vchunks(self.V_first), self.vchunks(self.V_own)
                for g_ in range(4):
                    p.dma("sp", self.msem("vb"), T["vb"][:, g_ * 4:(g_ + 1) * 4, :],
                          vf[g_].rearrange("(kt p) v -> p kt v", p=128)[:, :, h * 256:(h + 1) * 256],
                          reads=[self.firstV_bufs[g_]], writes=[B["m_vb"]])
                for g_ in range(grp + 1):
                    p.dma("sp", self.msem("vb"), T["vb"][:, 16 + g_ * 4:16 + (g_ + 1) * 4, :],
                          vo[g_].rearrange("(kt p) v -> p kt v", p=128)[:, :, h * 256:(h + 1) * 256],
                          reads=[self.kvbuf], writes=[B["m_vb"]])
                accb = [B["ps%d" % i] for i in range(6)]
                for kt in range(nkt):
                    jd = kt - (16 + grp * 4)
                    c0 = jd * 128 if jd > 0 else 0
                    kb = T["kb"][:, kt:kt + 1]
                    cur = []
                    for m in range(2):
                        qc = m * 6 + h
                        ps, psb = self.bank(self.free_banks)
                        p.op("pe", I("matmul", out=ps[:, c0:512], lhsT=T["kt"][:, m, kt * 128:(kt + 1) * 128], rhs=T["QT"][:, qc, c0:512],
                                     start=True, stop=True), reads=[B["m_kt"], B["m_QT"]], writes=[psb])
                        pa, pab = pas[pai % 3]
                        pai += 1
                        if jd >= 0:
                            mt, mtb = T["mt%d" % m], B["m_mt%d" % m]
                            p.op("dve", I("tensor_tensor", out=mt[:], in0=ps[:, c0:c0 + 128], in1=self.cst[:, 1, :], op=ALU.add),
                                 reads=[psb, B["cst"]], writes=[mtb])
                            fns = [I("activation", out=pa[:, c0:c0 + 128], in_=mt[:], func=AF.Exp, scale=sc, bias=kb)]
                            if c0 + 128 < 512:
                                fns.append(I("activation", out=pa[:, c0 + 128:512], in_=ps[:, c0 + 128:512], func=AF.Exp, scale=sc, bias=kb))
                            p.op("act", fns, reads=[mtb, psb, B["m_kb"]], writes=[pab])
                        else:
                            p.op("act", I("activation", out=pa[:], in_=ps[:], func=AF.Exp, scale=sc, bias=kb),
                                 reads=[psb, B["m_kb"]], writes=[pab])
                        cur.append((pa, pab))
                    st_, sp_ = (kt == 0), (kt == nkt - 1)
                    for m in range(2):
                        pa, pab = cur[m]
                        fns = [I("matmul", out=self.psb[m * 3 + vc][:, c0:512], lhsT=T["vb"][:, kt, vc * 128:(vc + 1) * 128],
                                 rhs=pa[:, c0:512], start=st_, stop=sp_) for vc in range(2)]
                        fns.append(I("matmul", out=self.psb[m * 3 + 2][:, c0:512], lhsT=self.ones_1[:], rhs=pa[:, c0:512], start=st_, stop=sp_))
                        p.op("pe", fns, reads=[B["m_vb"], pab, B["ones"]], writes=accb[m * 3:m * 3 + 3])
                p.op("dve", [I("reciprocal", out=T["rl0"][:], in_=self.psb[2][:]),
                             I("reciprocal", out=T["rl1"][:], in_=self.psb[5][:])],
                     reads=[accb[2], accb[5]], writes=[B["m_rl0"], B["m_rl1"]])
                p.op("dve", I("tensor_scalar", out=T["rl1"][:], in0=T["rl1"][:], scalar1=neglam, scalar2=None, op0=ALU.mult),
                     reads=[B["m_rl1"], B["m_lsc"]], writes=[B["m_rl1"]])
                for vc in range(2):
                    p.op("dve", [I("tensor_tensor", out=T["ofp"][:, vc, :], in0=self.psb[vc][:], in1=T["rl0"][:], op=ALU.mult),
                                 I("tensor_tensor", out=T["t1"][:], in0=self.psb[3 + vc][:], in1=T["rl1"][:], op=ALU.mult)],
                         reads=[accb[vc], accb[3 + vc], B["m_rl0"], B["m_rl1"]], writes=[B["m_ofp"], B["m_t1"]])
                    p.op("dve", I("tensor_tensor", out=T["ofp"][:, vc, :], in0=T["ofp"][:, vc, :], in1=T["t1"][:], op=ALU.add),
                         reads=[B["m_ofp"], B["m_t1"]], writes=[B["m_ofp"]])
                p.op("act", I("activation", out=T["osq"][:], in_=T["ofp"][:], func=AF.Square), reads=[B["m_ofp"]], writes=[B["m_osq"]])
                ps2, ps2b = self.bank(self.free_banks)
                p.op("pe", [I("matmul", out=ps2[:], lhsT=self.ones_v[:], rhs=T["osq"][:, vc, :], start=(vc == 0), stop=(vc == 1))
                            for vc in range(2)], reads=[B["m_osq"], B["ones"]], writes=[ps2b])
                self.rsqrt(ps2[:], ps2b, T["hrl"][:], B["m_hrl"], 512, post=1.0 - lam_init)
                for vc in range(2):
                    gc = VOFF["b_subln_g"] + j * 2 + vc
                    p.op("dve", I("scalar_tensor_tensor", out=self.mixT(h * 2 + vc), in0=T["ofp"][:, vc, :],
                                  scalar=self.vec_sb[:, gc:gc + 1], in1=T["hrl"][:], op0=ALU.mult, op1=ALU.mult),
                         reads=[B["m_ofp"], B["m_hrl"], B["vecs"]], writes=[self.mixbuf[h * 2 + vc]])
            self.tail(L, grp)


L1_STEPS = [("ffn", 0, 0), ("mixA", 0), ("ffn", 0, 1), ("ffn", 1, 0), ("mixA", 1), ("ffn", 1, 1), ("kv",)]
L2_STEPS = [("ffn", 2, 0), ("mixB", 2), ("ffn", 2, 1), ("ffn", 3, 0), ("mixB", 3), ("ffn", 3, 1)]
NCORES = 8


def common_maps(inp, xTs):
    f32 = lambda a: np.ascontiguousarray(np.asarray(a, np.float32))
    vecs = pack_vecs(inp)
    consts = make_consts()
    mem = f32(inp["mem"])
    shared = {"ffn_w_gu": f32(inp["ffn_w_gu"]), "ffn_w_down": f32(inp["ffn_w_down"]), "w_out": f32(inp["w_out"]),
              "mem_w_kv": f32(inp["mem_w_kv"]), "vecs": vecs, "consts": consts}
    maps = []
    for c in range(NCORES):
        b = c // 2
        m = dict(shared)
        m["xT"] = xTs[c]
        m["memT"] = np.ascontiguousarray(mem[b].T)
        maps.append(m)
    return maps


def a_inputs(inp):
    f32 = lambda a: np.ascontiguousarray(np.asarray(a, np.float32))
    return {"a_w_in": f32(inp["a_w_in"]),
            "a_vg_rep": np.ascontiguousarray(np.broadcast_to(f32(inp["a_v_norm_g"])[:, None, :], (N_A, 128, MIXW))),
            "a_wsT": np.ascontiguousarray(f32(inp["a_w_s"]).transpose(0, 3, 1, 2)),
            "a_bs": f32(inp["a_b_s"]).reshape(N_A, 1, 768)}


def b_inputs(inp, half):
    f32 = lambda a: np.ascontiguousarray(np.asarray(a, np.float32))
    kb = np.zeros((128, 32), np.float32)
    if half == 0:
        kb[:, 0:16] = -30000.0
    return {"b_w_in": f32(inp["b_w_in"]),
            "lamrep": np.ascontiguousarray(np.broadcast_to(f32(inp["b_lambda"])[:, None], (2, 128, 4, 128))),
            "kbias": kb}


_ROPE = {}


def rope_for(half):
    if half not in _ROPE:
        _ROPE[half] = rope_tables_T(half * TPC, TPC)
    return _ROPE[half]


def kernel_unfused(**inputs):
    x = np.asarray(inputs["x"], np.float32)
    xTs = [np.ascontiguousarray(x[c // 2, (c % 2) * TPC:(c % 2 + 1) * TPC].T) for c in range(NCORES)]
    maps = common_maps(inputs, xTs)
    ai = a_inputs(inputs)
    for c in range(NCORES):
        maps[c].update(ai)
        maps[c]["ropeT"] = rope_for(c % 2)
        maps[c]["w_kv"] = np.ascontiguousarray(np.asarray(inputs["w_kv"], np.float32))
    nc1 = K({"steps": L1_STEPS, "kvmode": "out"}).build()
    r1 = run_bass_kernel_spmd(nc1, maps, core_ids=list(range(NCORES))).results
    xTs2 = [r1[c]["oT"] for c in range(NCORES)]
    maps = common_maps(inputs, xTs2)
    for c in range(NCORES):
        maps[c].update(b_inputs(inputs, c % 2))
        maps[c]["ropeT"] = rope_for(c % 2)
        maps[c]["KT_own"] = r1[c]["KT_own"]
        maps[c]["V_own"] = r1[c]["V_own"]
        maps[c]["KT_first"] = r1[c - (c % 2)]["KT_own"]
        maps[c]["V_first"] = r1[c - (c % 2)]["V_own"]
    nc2 = K({"steps": L2_STEPS, "kvmode": "in"}).build()
    r2 = run_bass_kernel_spmd(nc2, maps, core_ids=list(range(NCORES))).results
    out = np.empty_like(x)
    for c in range(NCORES):
        out[c // 2, (c % 2) * TPC:(c % 2 + 1) * TPC] = r2[c]["oT"].T
    return out


FUSED_STEPS = [("first", s_) for s_ in L1_STEPS] + [("own", s_) for s_ in L1_STEPS] + [("own", s_) for s_ in L2_STEPS]


def kernel_recompute(**inputs):
    x = np.asarray(inputs["x"], np.float32)
    xTs = [np.ascontiguousarray(x[c // 2, (c % 2) * TPC:(c % 2 + 1) * TPC].T) for c in range(NCORES)]
    maps = common_maps(inputs, xTs)
    ai = a_inputs(inputs)
    w_kv = np.ascontiguousarray(np.asarray(inputs["w_kv"], np.float32))
    for c in range(NCORES):
        maps[c].update(ai)
        maps[c].update(b_inputs(inputs, c % 2))
        maps[c]["ropeT"] = rope_for(c % 2)
        maps[c]["ropeT_first"] = rope_for(0)
        maps[c]["xT_first"] = xTs[c - (c % 2)]
        maps[c]["w_kv"] = w_kv
    nc = K({"steps": FUSED_STEPS, "kvmode": "fused"}).build()
    r = run_bass_kernel_spmd(nc, maps, core_ids=list(range(NCORES))).results
    out = np.empty_like(x)
    for c in range(NCORES):
        out[c // 2, (c % 2) * TPC:(c % 2 + 1) * TPC] = r[c]["oT"].T
    return out


XCHG_STEPS = L1_STEPS + [("xchg",)] + L2_STEPS


def kernel(**inputs):
    x = np.asarray(inputs["x"], np.float32)
    xTs = [np.ascontiguousarray(x[c // 2, (c % 2) * TPC:(c % 2 + 1) * TPC].T) for c in range(NCORES)]
    maps = common_maps(inputs, xTs)
    ai = a_inputs(inputs)
    w_kv = np.ascontiguousarray(np.asarray(inputs["w_kv"], np.float32))
    for c in range(NCORES):
        maps[c].update(ai)
        maps[c].update(b_inputs(inputs, c % 2))
        maps[c]["ropeT"] = rope_for(c % 2)
        maps[c]["w_kv"] = w_kv
    nc = K({"steps": XCHG_STEPS, "kvmode": "xchg"}).build()
    r = run_bass_kernel_spmd(nc, maps, core_ids=list(range(NCORES))).results
    out = np.empty_like(x)
    for c in range(NCORES):
        out[c // 2, (c % 2) * TPC:(c % 2 + 1) * TPC] = r[c]["oT"].T
    return out
```
